# Optimizing a Trainium2 kernel written in Bass

```python
import jax
import jax.numpy as jnp
from jax import lax
import numpy as np


D_MODEL = 2048
BATCH = 2
SEQ = 16384
DEPTH = 4

GRID_W = 64
CTX_LEN = 256

GLA_HEADS = 4
GLA_DK = 256
GLA_DV = 256
GLA_KEY_W = GLA_HEADS * GLA_DK
GLA_VAL_W = GLA_HEADS * GLA_DV
GLA_GATE_RANK = 16
GLA_TAU = 16.0
GLA_CHUNK = 64

CONV_W = D_MODEL // 2
CONV_K = 3

POOL_W = D_MODEL
POOL_WINDOWS = (2, 4, 8, 16)
POOL_GROUP = POOL_W // len(POOL_WINDOWS)

E_SIZES = (GLA_KEY_W, GLA_VAL_W, GLA_GATE_RANK, GLA_GATE_RANK, GLA_KEY_W, GLA_VAL_W,
           CONV_W, CONV_W, CONV_W, CONV_W)
E_IN = sum(E_SIZES)
E_STATE_COLS = GLA_KEY_W + GLA_VAL_W + 2 * GLA_GATE_RANK
E_MIX = GLA_VAL_W + CONV_W
O_IN = 2 * POOL_W
LN_EPS = 1e-5

kernel_name = 'hybrid_gla_shortconv_pool_deepnorm_prefix'


def layer_norm(x, g, b):
    xf = x.astype(jnp.float32)
    mu = jnp.mean(xf, axis=-1, keepdims=True)
    var = jnp.mean(jnp.square(xf - mu), axis=-1, keepdims=True)
    return ((xf - mu) * lax.rsqrt(var + LN_EPS)).astype(x.dtype) * g + b


def rms_norm(x, g):
    xf = x.astype(jnp.float32)
    return xf * lax.rsqrt(jnp.mean(jnp.square(xf), axis=-1, keepdims=True) + LN_EPS) * g.astype(jnp.float32)


def modulation(cond, w, b, n):
    m = jax.nn.silu(cond) @ w[:, :n * D_MODEL] + b[:n * D_MODEL]
    return jnp.split(m, n, axis=-1)


def split_cols(h, sizes):
    idx = [int(i) for i in np.cumsum(sizes)[:-1]]
    return jnp.split(h, idx, axis=-1)


def to_heads(t, d):
    bn, L, _ = t.shape
    return t.reshape(bn, L, -1, d).transpose(0, 2, 1, 3)


def from_heads(t):
    bn, H, L, d = t.shape
    return t.transpose(0, 2, 1, 3).reshape(bn, L, H * d)


def flip_seq(t):
    return jnp.flip(t, axis=2)


def gate_log(r, w, b):
    return jax.nn.log_sigmoid((r @ w + b).astype(jnp.float32)) / GLA_TAU


def gla_chunked(q, k, v, log_a, s0):
    bn, H, L, _ = q.shape
    dv = v.shape[-1]
    C = GLA_CHUNK
    nc = L // C

    def to_chunks(t):
        return jnp.moveaxis(t.astype(jnp.float32).reshape(bn, H, nc, C, t.shape[-1]), 2, 0)

    lower = np.tril(np.ones((C, C), dtype=bool))[:, :, None]

    def step(s, inp):
        qc, kc, vc, lac = inp
        g = jnp.cumsum(lac, axis=-2)
        diff = g[..., :, None, :] - g[..., None, :, :]
        decay = jnp.exp(jnp.where(lower, diff, -jnp.inf))
        scores = jnp.einsum('bhik,bhjk,bhijk->bhij', qc, kc, decay)
        o = (jnp.einsum('bhij,bhjv->bhiv', scores, vc)
             + jnp.einsum('bhik,bhkv->bhiv', qc * jnp.exp(g), s))
        g_last = g[..., -1:, :]
        s_new = (jnp.exp(g_last)[..., 0, :, None] * s
                 + jnp.einsum('bhjk,bhjv->bhkv', kc * jnp.exp(g_last - g), vc))
        return s_new, o

    s_fin, o = lax.scan(step, s0, (to_chunks(q), to_chunks(k), to_chunks(v), to_chunks(log_a)))
    o = jnp.moveaxis(o, 0, 2).reshape(bn, H, L, dv)
    return o, s_fin


def gla_final_state(k, v, log_a):
    g = jnp.cumsum(log_a, axis=-2)
    return jnp.einsum('bhtk,bhtv->bhkv', k * jnp.exp(g[..., -1:, :] - g), v)


def context_states(hs, w_gf, b_gf, w_gb, b_gb):
    k, v, rf, rb = split_cols(hs, E_SIZES[:4])
    k = to_heads(k, GLA_DK).astype(jnp.float32)
    v = to_heads(v, GLA_DV).astype(jnp.float32)
    la_f = to_heads(gate_log(rf, w_gf, b_gf), GLA_DK)
    la_b = to_heads(gate_log(rb, w_gb, b_gb), GLA_DK)
    s_f = gla_final_state(k, v, la_f)
    s_b = gla_final_state(flip_seq(k), flip_seq(v), flip_seq(la_b))
    return s_f, s_b


def conv3_centred(x, w):
    zero = jnp.zeros_like(x[..., :1, :])
    x_prev = jnp.concatenate([zero, x[..., :-1, :]], axis=-2)
    x_next = jnp.concatenate([x[..., 1:, :], zero], axis=-2)
    return w[0] * x_prev + w[1] * x + w[2] * x_next


def centred_window_mean(x, window, axis):
    L = x.shape[axis]
    xf = jnp.moveaxis(x, axis, 0).astype(jnp.float32)
    cs = jnp.concatenate([jnp.zeros_like(xf[:1]), jnp.cumsum(xf, axis=0)], axis=0)
    t = np.arange(L)
    lo = np.clip(t - window // 2, 0, L)
    hi = np.clip(t + window - window // 2, 0, L)
    cnt = (hi - lo).astype(np.float32).reshape((L,) + (1,) * (xf.ndim - 1))
    mean = (cs[hi] - cs[lo]) / cnt
    return jnp.moveaxis(mean, 0, axis).astype(x.dtype)


def even_mixer(h, s0_f, s0_b, w_gf, b_gf, w_gb, b_gb, gla_norm_w, conv_w, w_out, grid):
    bn, L, _ = h.shape
    k, v, rf, rb, q, gate_b, a_b, a_c, a_x, gate_a = split_cols(h, E_SIZES)
    qh = to_heads(q * (GLA_DK ** -0.5), GLA_DK)
    kh = to_heads(k, GLA_DK)
    vh = to_heads(v, GLA_DV)
    la_f = to_heads(gate_log(rf, w_gf, b_gf), GLA_DK)
    la_b = to_heads(gate_log(rb, w_gb, b_gb), GLA_DK)
    o_f, s_f = gla_chunked(qh, kh, vh, la_f, s0_f)
    o_b, s_b = gla_chunked(flip_seq(qh), flip_seq(kh), flip_seq(vh), flip_seq(la_b), s0_b)
    o = rms_norm(o_f + flip_seq(o_b), gla_norm_w)
    y_b = from_heads(o).astype(h.dtype) * jax.nn.silu(gate_b)
    u = (a_c * a_x).reshape(bn, grid[0], grid[1], CONV_W)
    u = conv3_centred(u, conv_w).reshape(bn, L, CONV_W)
    y_a = a_b * u * jax.nn.silu(gate_a)
    return jnp.concatenate([y_b, y_a], axis=-1) @ w_out, s_f, s_b


def odd_mixer(h, w_pool, pool_scale, w_out, grid):
    bn, L, _ = h.shape
    xp, gate = jnp.split(h, 2, axis=-1)
    xg = xp.reshape(bn, grid[0], grid[1], POOL_W)
    groups = jnp.split(xg, len(POOL_WINDOWS), axis=-1)
    z = jnp.stack([centred_window_mean(g, w, 1) - g for g, w in zip(groups, POOL_WINDOWS)], axis=-2)
    z = jnp.einsum('brcgi,gio->brcgo', z, w_pool).reshape(bn, L, POOL_W) * pool_scale
    return (z * jax.nn.silu(gate)) @ w_out


def setup_inputs(seed: int = 0) -> dict:
    key = jax.random.key(seed)
    ks = jax.random.split(key, 20)
    ne = (DEPTH + 1) // 2
    no = DEPTH // 2
    beta = (8.0 * DEPTH) ** -0.25

    def nrm(k, shape, s):
        return jax.random.normal(k, shape, jnp.float32) * s

    return {
        'x': nrm(ks[0], (BATCH, SEQ, D_MODEL), 1.0),
        'c': nrm(ks[1], (BATCH, D_MODEL), 1.0),
        'ctx': nrm(ks[2], (BATCH, CTX_LEN, D_MODEL), 1.0),
        'c_ctx': nrm(ks[3], (D_MODEL,), 1.0),
        'w_ada': nrm(ks[4], (DEPTH, D_MODEL, 3 * D_MODEL), 0.5 * D_MODEL ** -0.5),
        'b_ada': nrm(ks[5], (DEPTH, 3 * D_MODEL), 0.02),
        'ln_g': 1.0 + nrm(ks[6], (DEPTH, D_MODEL), 0.02),
        'ln_b': nrm(ks[7], (DEPTH, D_MODEL), 0.02),
        'w_in_e': nrm(ks[8], (ne, D_MODEL, E_IN), D_MODEL ** -0.5),
        'w_gate_f': nrm(ks[9], (ne, GLA_GATE_RANK, GLA_KEY_W), GLA_GATE_RANK ** -0.5),
        'b_gate_f': nrm(ks[10], (ne, GLA_KEY_W), 0.1),
        'w_gate_b': nrm(ks[11], (ne, GLA_GATE_RANK, GLA_KEY_W), GLA_GATE_RANK ** -0.5),
        'b_gate_b': nrm(ks[12], (ne, GLA_KEY_W), 0.1),
        'gla_norm_w': 1.0 + nrm(ks[13], (ne, GLA_DV), 0.02),
        'conv_w': nrm(ks[14], (ne, CONV_K, CONV_W), CONV_K ** -0.5),
        'w_out_e': nrm(ks[15], (ne, E_MIX, D_MODEL), beta * E_MIX ** -0.5),
        'w_in_o': nrm(ks[16], (no, D_MODEL, O_IN), D_MODEL ** -0.5),
        'w_pool': nrm(ks[17], (no, len(POOL_WINDOWS), POOL_GROUP, POOL_GROUP), POOL_GROUP ** -0.5),
        'pool_scale': 1.0 + nrm(ks[18], (no, POOL_W), 0.02),
        'w_out_o': nrm(ks[19], (no, POOL_W, D_MODEL), beta * POOL_W ** -0.5),
    }


def reference(x, c, ctx, c_ctx, w_ada, b_ada, ln_g, ln_b, w_in_e, w_gate_f, b_gate_f,
              w_gate_b, b_gate_b, gla_norm_w, conv_w, w_out_e, w_in_o, w_pool, pool_scale, w_out_o):
    alpha = (2.0 * DEPTH) ** 0.25
    bn, L, _ = x.shape
    rows = L // GRID_W
    lc = ctx.shape[1]
    c_lat = c[:, None, :]
    c_c = c_ctx[None, None, :]
    s_zero = jnp.zeros((bn, GLA_HEADS, GLA_DK, GLA_DV), jnp.float32)
    ctx_s = ctx
    for i in range(DEPTH):
        j = i // 2
        ctx_needed = any(l % 2 == 0 for l in range(i + 1, DEPTH))
        sh, sc, gt = modulation(c_lat, w_ada[i], b_ada[i], 3)
        u = x * (1.0 + sc) + sh
        if i % 2 == 0:
            if ctx_needed:
                sh_c, sc_c, gt_c = modulation(c_c, w_ada[i], b_ada[i], 3)
                uc = ctx_s * (1.0 + sc_c) + sh_c
                yc, s_f, s_b = even_mixer(uc @ w_in_e[j], s_zero, s_zero, w_gate_f[j], b_gate_f[j],
                                          w_gate_b[j], b_gate_b[j], gla_norm_w[j], conv_w[j],
                                          w_out_e[j], (1, lc))
                ctx_s = layer_norm(alpha * ctx_s + gt_c * yc, ln_g[i], ln_b[i])
            else:
                sh_c, sc_c = modulation(c_c, w_ada[i], b_ada[i], 2)
                uc = ctx_s * (1.0 + sc_c) + sh_c
                s_f, s_b = context_states(uc @ w_in_e[j][:, :E_STATE_COLS], w_gate_f[j], b_gate_f[j],
                                          w_gate_b[j], b_gate_b[j])
            y, _, _ = even_mixer(u @ w_in_e[j], s_f, s_b, w_gate_f[j], b_gate_f[j], w_gate_b[j],
                                 b_gate_b[j], gla_norm_w[j], conv_w[j], w_out_e[j], (rows, GRID_W))
        else:
            if ctx_needed:
                sh_c, sc_c, gt_c = modulation(c_c, w_ada[i], b_ada[i], 3)
                uc = ctx_s * (1.0 + sc_c) + sh_c
                yc = odd_mixer(uc @ w_in_o[j], w_pool[j], pool_scale[j], w_out_o[j], (lc, 1))
                ctx_s = layer_norm(alpha * ctx_s + gt_c * yc, ln_g[i], ln_b[i])
            y = odd_mixer(u @ w_in_o[j], w_pool[j], pool_scale[j], w_out_o[j], (rows, GRID_W))
        x = layer_norm(alpha * x + gt * y, ln_g[i], ln_b[i])
    return x
```

```python
import contextlib
import numpy as np
import ml_dtypes
import concourse.bass as bass
import concourse.mybir as mybir
from concourse.bass_utils import run_bass_kernel_spmd

F32 = mybir.dt.float32
BF16 = mybir.dt.bfloat16
AF = mybir.ActivationFunctionType
ALU = mybir.AluOpType
NPBF = ml_dtypes.bfloat16

D = 2048
NB = 2
SEQ = 16384
DEPTH = 4
GRID_W = 64
CTX = 256
NCORE = 8
H = 4
DK = 256
EPS = 1e-5
ALPHA = (2.0 * DEPTH) ** 0.25
POOL_WINDOWS = (2, 4, 8, 16)


class Buf:
    __slots__ = ("w", "r", "key")

    def __init__(self):
        self.w = {}
        self.r = {}
        self.key = None


class Eng:
    def __init__(self, key, h, sem):
        self.key = key
        self.h = h
        self.sem = sem
        self.n = 0
        self.seen = {}


class Launch:
    def __init__(self, name):
        self.name = name
        self.nc = bass.Bass("TRN2", target_bir_lowering=False)
        self.es = contextlib.ExitStack()
        nc = self.nc
        self.sems = {}
        self.E = {}
        for key, h in (("pe", nc.tensor), ("act", nc.scalar), ("dve", nc.vector),
                       ("pool", nc.gpsimd), ("sp", nc.sync)):
            sem = self.es.enter_context(nc.semaphore("s_" + key))
            self.sems[key] = sem
            self.E[key] = Eng(key, h, sem)
        self.streams = {}
        self.in_names = []
        self.out_names = []
        self.ntile = 0
        self.nbuf = 0

    def din(self, name, shape, dt):
        self.in_names.append(name)
        return self.nc.dram_tensor(name, list(shape), dt, kind="ExternalInput").ap()

    def dout(self, name, shape, dt):
        self.out_names.append(name)
        return self.nc.dram_tensor(name, list(shape), dt, kind="ExternalOutput").ap()

    def dint(self, name, shape, dt):
        return self.nc.dram_tensor(name, list(shape), dt, kind="Internal").ap()

    def sb(self, shape, dt, name=None):
        self.ntile += 1
        return self.es.enter_context(self.nc.sbuf_tensor(name or f"t{self.ntile}", list(shape), dt))

    def ps(self, shape, dt, name=None):
        self.ntile += 1
        return self.es.enter_context(self.nc.psum_tensor(name or f"p{self.ntile}", list(shape), dt))

    def stream(self, key):
        if key not in self.streams:
            sem = self.es.enter_context(self.nc.semaphore("d_" + key))
            self.sems["d_" + key] = sem
            self.streams[key] = [sem, 0]
        return key

    def _waits(self, e, R, W):
        need = {}
        for b in R:
            for k, v in b.w.items():
                if need.get(k, 0) < v:
                    need[k] = v
        for b in W:
            for k, v in b.w.items():
                if need.get(k, 0) < v:
                    need[k] = v
            for k, v in b.r.items():
                if need.get(k, 0) < v:
                    need[k] = v
        for k, v in need.items():
            if e.key == "pe" and k == "pe":
                continue
            if e.seen.get(k, 0) < v:
                e.h.wait_ge(self.sems[k], v)
                e.seen[k] = v

    def op(self, en, emit, R=(), W=(), inc=True):
        e = self.E[en]
        self._waits(e, R, W)
        inst = emit(e.h)
        if inc:
            e.n += 1
            inst.then_inc(e.sem, 1)
            tok = e.n
        else:
            tok = e.n + 1
        for b in R:
            if b.r.get(e.key, 0) < tok:
                b.r[e.key] = tok
        for b in W:
            b.w = {e.key: tok}
            b.r = {}
        return inst

    def dma(self, stream, out, in_, R=(), W=(), q="sp", sem_of=None):
        e = self.E[q]
        kb = sem_of if sem_of is not None else (W[0] if W else (R[0] if R else None))
        if kb is not None:
            if kb.key is None:
                self.nbuf += 1
                kb.key = f"b{self.nbuf}"
            stream = kb.key
        self.stream(stream)
        st = self.streams[stream]
        self._waits(e, R, W)
        inst = e.h.dma_start(out=out, in_=in_)
        st[1] += 1
        inst.then_inc(st[0], 16)
        tok = 16 * st[1]
        k = "d_" + stream
        for b in R:
            if b.r.get(k, 0) < tok:
                b.r[k] = tok
        for b in W:
            b.w = {k: tok}
            b.r = {}
        return inst

    def finish(self):
        sp = self.E["sp"]
        for key, (sem, n) in self.streams.items():
            if n > 0:
                sp.h.wait_ge(sem, 16 * n)
        for k in ("act", "dve", "pool", "pe"):
            e = self.E[k]
            if e.n > 0:
                sp.h.wait_ge(e.sem, e.n)

    def run(self, in_maps):
        self.finish()
        res = run_bass_kernel_spmd(self.nc, in_maps, core_ids=list(range(len(in_maps))))
        self.es.close()
        return res.results


class Ring:
    def __init__(self, L, n, shape, dt, psum=False):
        self.items = []
        for _ in range(n):
            t = L.ps(shape, dt) if psum else L.sb(shape, dt)
            self.items.append((t, Buf()))
        self.i = 0

    def next(self):
        it = self.items[self.i % len(self.items)]
        self.i += 1
        return it


def act(L, out, in_, func, R, W, scale=None, bias=None):
    kw = {}
    if scale is not None:
        kw["scale"] = scale
    if bias is not None:
        kw["bias"] = bias
    return L.op("act", lambda h: h.activation(out=out, in_=in_, func=func, **kw), R, W)


def tt(L, en, out, a, b, op, R, W):
    return L.op(en, lambda h: h.tensor_tensor(out=out, in0=a, in1=b, op=op), R, W)


def ts(L, en, out, a, s1, op0, R, W, s2=None, op1=None):
    if op1 is None:
        return L.op(en, lambda h: h.tensor_scalar(out=out, in0=a, scalar1=s1, scalar2=None, op0=op0), R, W)
    return L.op(en, lambda h: h.tensor_scalar(out=out, in0=a, scalar1=s1, scalar2=s2, op0=op0, op1=op1), R, W)


def stt(L, out, a, s, b, op0, op1, R, W):
    return L.op("dve", lambda h: h.scalar_tensor_tensor(out=out, in0=a, scalar=s, in1=b, op0=op0, op1=op1), R, W)


def mm(L, out, lhsT, rhs, start, stop, R, W):
    return L.op("pe", lambda h: h.matmul(out, lhsT=lhsT, rhs=rhs, start=start, stop=stop), R, W, inc=stop)


class WPipe:
    def __init__(self, L, nstage, nbf, kc=16, cols=128, pf=4):
        self.L = L
        self.stage = Ring(L, nstage, [128, kc, cols], F32)
        self.bf = Ring(L, nbf, [128, kc, cols], BF16)
        self.plan = []
        self.ready = []
        self.taken = 0
        self.pf = pf

    def set_plan(self, srcs):
        self.plan = list(srcs)

    def get(self):
        while len(self.ready) < min(len(self.plan), self.taken + 1 + self.pf):
            self.ready.append(self.fetch(self.plan[len(self.ready)]))
        r = self.ready[self.taken]
        self.taken += 1
        return r

    def fetch(self, src_ap):
        L = self.L
        st, sb_ = self.stage.next()
        L.dma("w", st[:], src_ap, R=(), W=(sb_,))
        wt, wb = self.bf.next()
        L.op("pool", lambda h: h.tensor_copy(out=wt[:], in_=st[:]), R=(sb_,), W=(wb,))
        return wt, wb


def build_mod():
    L = Launch("mod")
    nblk = DEPTH * 6
    cT = L.din("cT", [128, 16, 4], F32)
    wada = L.din("wada", [nblk, 128, 16, 128], F32)
    bada = L.din("bada", [128, nblk], F32)
    modT = L.dout("modT", [128, nblk, 4], F32)
    ct = L.sb([128, 16, 4], F32)
    st = L.sb([128, 16, 4], F32)
    bt = L.sb([128, nblk], F32)
    ot = L.sb([128, nblk, 4], F32)
    b_c, b_s, b_b, b_o = Buf(), Buf(), Buf(), Buf()
    L.dma("ld", ct[:], cT, W=(b_c,))
    L.dma("ld", bt[:], bada, W=(b_b,))
    act(L, st[:], ct[:], AF.Silu, R=(b_c,), W=(b_s,))
    wr = Ring(L, 3, [128, 16, 128], F32)
    pr = Ring(L, 2, [128, 512], F32, psum=True)
    for blk in range(nblk):
        wt, wb = wr.next()
        L.dma("w", wt[:], wada[blk], W=(wb,))
        pt, pb = pr.next()
        for kc in range(16):
            mm(L, pt[:, 0:4], wt[:, kc, :], st[:, kc, :], kc == 0, kc == 15, R=(wb, b_s), W=(pb,))
        act(L, ot[:, blk, :], pt[:, 0:4], AF.Identity, R=(pb, b_b), W=(b_o,), bias=bt[:, blk:blk + 1])
    L.dma("st", modT, ot[:], R=(b_o,))
    return L


def load_mod(L, nstream):
    mod = L.din("mod", [128, nstream, 3, 16], F32)
    mt = L.sb([128, nstream, 3, 16], F32)
    b = Buf()
    L.dma("ld", mt[:], mod, W=(b,))
    sc1 = L.sb([128, nstream, 16], F32)
    gta = L.sb([128, nstream, 16], F32)
    b1, b2 = Buf(), Buf()
    ts(L, "dve", sc1[:], mt[:, :, 1, :], 1.0, ALU.add, R=(b,), W=(b1,))
    ts(L, "dve", gta[:], mt[:, :, 2, :], 1.0 / ALPHA, ALU.mult, R=(b,), W=(b2,))
    return mt, b, sc1, b1, gta, b2


def token_tiles(Ts, tile=512):
    out = []
    for s, T in enumerate(Ts):
        t0 = 0
        while t0 < T:
            n = min(tile, T - t0)
            out.append((s, t0, n))
            t0 += n
    return out


def make_passes(tiles, cap):
    passes, cur, tot = [], [], 0
    for t in tiles:
        if tot + t[2] > cap and cur:
            passes.append(cur)
            cur, tot = [], 0
        cur.append(t)
        tot += t[2]
    if cur:
        passes.append(cur)
    return passes


def build_uT(L, uT, ub, xTs, tiles, mt, mb, sc1, sb1, xr):
    off = 0
    for ti, (s, t0, n) in enumerate(tiles):
        xv = xTs[s].rearrange("(c p) t -> p c t", p=128)
        for g in range(4):
            xt, xb = xr.next()
            L.dma("x", xt[:, :, 0:n], xv[:, g * 4:(g + 1) * 4, t0:t0 + n], W=(xb,))
            for c in range(4):
                fc = g * 4 + c
                act(L, uT[:, fc, off:off + n], xt[:, c, 0:n], AF.Identity, R=(xb, mb, sb1), W=(ub[ti][fc],),
                    scale=sc1[:, s, fc:fc + 1], bias=mt[:, s, 0, fc:fc + 1])
        off += n


def build_A_even(Ts, rowlens):
    L = Launch("A_even")
    ns = len(Ts)
    xTs = [L.din(f"xT{s}", [D, Ts[s]], F32) for s in range(ns)]
    wfm = L.din("wfm", [64, 128, 16, 128], F32)
    wr = L.din("wr", [128, 16, 32], F32)
    wg = L.din("wg", [16, 2, 1024], F32)
    bg = L.din("bg", [128, 2, 8], F32)
    convw = L.din("convw", [128, 3, 8], F32)
    o_kT = [L.dout(f"kT{s}", [1024, Ts[s]], F32) for s in range(ns)]
    o_qT = [L.dout(f"qT{s}", [1024, Ts[s]], F32) for s in range(ns)]
    o_gb = [L.dout(f"gbT{s}", [1024, Ts[s]], F32) for s in range(ns)]
    o_ya = [L.dout(f"yaT{s}", [1024, Ts[s]], BF16) for s in range(ns)]
    o_v = [L.dout(f"v{s}", [Ts[s], 1024], BF16) for s in range(ns)]
    o_l = [[L.dout(f"l{'fb'[d]}T{s}", [1024, Ts[s]], F32) for s in range(ns)] for d in range(2)]
    mt, mb, sc1, sb1, gta, gb2 = load_mod(L, ns)

    wr_f = L.sb([128, 16, 32], F32)
    wr_b = L.sb([128, 16, 32], BF16)
    wg_t = L.sb([16, 2, 1024], F32)
    bg_t = L.sb([128, 2, 8], F32)
    nbg = L.sb([128, 2, 8], F32)
    cw = L.sb([128, 3, 8], F32)
    b_wrf, b_wr, b_wg, b_bg, b_nbg, b_cw = Buf(), Buf(), Buf(), Buf(), Buf(), Buf()
    L.dma("ld", wr_f[:], wr, W=(b_wrf,))
    L.dma("ld", wg_t[:], wg, W=(b_wg,))
    L.dma("ld", bg_t[:], bg, W=(b_bg,))
    L.dma("ld", cw[:], convw, W=(b_cw,))
    L.op("pool", lambda h: h.tensor_copy(out=wr_b[:], in_=wr_f[:]), R=(b_wrf,), W=(b_wr,))
    ts(L, "dve", nbg[:], bg_t[:], -1.0, ALU.mult, R=(b_bg,), W=(b_nbg,))

    tiles = token_tiles(Ts)
    passes = make_passes(tiles, 1536)
    CAP = max(sum(t[2] for t in p) for p in passes)
    uT = L.sb([128, 16, CAP], BF16)
    xr = Ring(L, 2, [128, 4, 512], F32)
    wp = WPipe(L, 2, 8)
    pr = Ring(L, 8, [128, 512], F32, psum=True)
    ob = Ring(L, 4, [128, 512], BF16)
    rT = Ring(L, 2, [16, 2, 512], F32)
    el = Ring(L, 3, [128, 512], F32)
    cvt = Ring(L, 8, [128, 512], F32)
    ob32 = Ring(L, 3, [128, 512], F32)

    wp.set_plan([wfm[blk] for _ in passes for blk in range(64)])
    ub_all = [[Buf() for _ in range(16)] for _ in range(max(len(p) for p in passes))]
    for tiles_p in passes:
        ub = ub_all[:len(tiles_p)]
        build_uT(L, uT, ub, xTs, tiles_p, mt, mb, sc1, sb1, xr)
        offs = np.cumsum([0] + [t[2] for t in tiles_p])
        for ti, (s, t0, n) in enumerate(tiles_p):
            o = int(offs[ti])
            rt, rb = rT.next()
            for d in range(2):
                pt, pb = pr.next()
                for kc in range(16):
                    mm(L, pt[0:16, 0:n], wr_b[:, kc, d * 16:(d + 1) * 16], uT[:, kc, o:o + n], kc == 0, kc == 15,
                       R=(b_wr, ub[ti][kc]), W=(pb,))
                act(L, rt[:, d, 0:n], pt[0:16, 0:n], AF.Identity, R=(pb,), W=(rb,))
            for d in range(2):
                for c in range(8):
                    pt, pb = pr.next()
                    mm(L, pt[:, 0:n], wg_t[:, d, c * 128:(c + 1) * 128], rt[:, d, 0:n], True, True, R=(b_wg, rb), W=(pb,))
                    et, eb = el.next()
                    act(L, et[:, 0:n], pt[:, 0:n], AF.Exp, R=(pb, b_nbg), W=(eb,), scale=-1.0, bias=nbg[:, d, c:c + 1])
                    lt, lb = el.next()
                    act(L, lt[:, 0:n], et[:, 0:n], AF.Ln, R=(eb,), W=(lb,), bias=1.0)
                    L.dma("st", o_l[d][s][c * 128:(c + 1) * 128, t0:t0 + n], lt[:, 0:n], R=(lb,))
        for blk in range(24):
            wt, wb = wp.get()
            kind, c = blk // 8, blk % 8
            for ti, (s, t0, n) in enumerate(tiles_p):
                o = int(offs[ti])
                pt, pb = pr.next()
                for kc in range(16):
                    mm(L, pt[:, 0:n], wt[:, kc, :], uT[:, kc, o:o + n], kc == 0, kc == 15, R=(wb, ub[ti][kc]), W=(pb,))
                st, sb_ = ob32.next()
                if kind == 0:
                    act(L, st[:, 0:n], pt[:, 0:n], AF.Identity, R=(pb,), W=(sb_,))
                    dst = o_kT[s]
                elif kind == 1:
                    act(L, st[:, 0:n], pt[:, 0:n], AF.Identity, R=(pb,), W=(sb_,), scale=DK ** -0.5)
                    dst = o_qT[s]
                else:
                    act(L, st[:, 0:n], pt[:, 0:n], AF.Silu, R=(pb,), W=(sb_,))
                    dst = o_gb[s]
                L.dma("st", dst[c * 128:(c + 1) * 128, t0:t0 + n], st[:, 0:n], R=(sb_,))
        for cc in range(8):
            ws = [wp.get() for g in range(4)]
            for ti, (s, t0, n) in enumerate(tiles_p):
                o = int(offs[ti])
                rl = rowlens[s]
                pp = []
                for g in range(4):
                    pt, pb = pr.next()
                    for kc in range(16):
                        mm(L, pt[:, 0:n], ws[g][0][:, kc, :], uT[:, kc, o:o + n], kc == 0, kc == 15,
                           R=(ws[g][1], ub[ti][kc]), W=(pb,))
                    pp.append((pt, pb))
                (pB, bB), (pC, bC), (pX, bX), (pG, bG) = pp
                ax, axb = cvt.next()
                act(L, ax[:, 0:n], pX[:, 0:n], AF.Identity, R=(bX,), W=(axb,))
                p_, p_b = cvt.next()
                tt(L, "dve", p_[:, 0:n], pC[:, 0:n], ax[:, 0:n], ALU.mult, R=(bC, axb), W=(p_b,))
                cv, cvb = cvt.next()
                ts(L, "pool", cv[:, 0:n], p_[:, 0:n], cw[:, 1, cc:cc + 1], ALU.mult, R=(p_b, b_cw), W=(cvb,))
                p3 = p_[:, 0:n].rearrange("p (r t) -> p r t", t=rl)
                c3 = cv[:, 0:n].rearrange("p (r t) -> p r t", t=rl)
                stt(L, c3[:, :, 1:rl], p3[:, :, 0:rl - 1], cw[:, 0, cc:cc + 1], c3[:, :, 1:rl], ALU.mult, ALU.add,
                    R=(p_b, b_cw, cvb), W=(cvb,))
                stt(L, c3[:, :, 0:rl - 1], p3[:, :, 1:rl], cw[:, 2, cc:cc + 1], c3[:, :, 0:rl - 1], ALU.mult, ALU.add,
                    R=(p_b, b_cw, cvb), W=(cvb,))
                sg, sgb = cvt.next()
                act(L, sg[:, 0:n], pG[:, 0:n], AF.Silu, R=(bG,), W=(sgb,))
                t1, t1b = cvt.next()
                tt(L, "dve", t1[:, 0:n], pB[:, 0:n], cv[:, 0:n], ALU.mult, R=(bB, cvb), W=(t1b,))
                st, sb_ = ob.next()
                tt(L, "pool", st[:, 0:n], t1[:, 0:n], sg[:, 0:n], ALU.mult, R=(t1b, sgb), W=(sb_,))
                L.dma("st", o_ya[s][cc * 128:(cc + 1) * 128, t0:t0 + n], st[:, 0:n], R=(sb_,))
        for vg in range(2):
            ws = [wp.get() for g in range(4)]
            for ti, (s, t0, n) in enumerate(tiles_p):
                o = int(offs[ti])
                for sub in range(n // 128):
                    pt, pb = pr.next()
                    for g in range(4):
                        for kc in range(16):
                            mm(L, pt[:, g * 128:(g + 1) * 128], uT[:, kc, o + sub * 128:o + (sub + 1) * 128],
                               ws[g][0][:, kc, :], kc == 0, kc == 15, R=(ws[g][1], ub[ti][kc]), W=(pb,))
                    st, sb_ = ob.next()
                    act(L, st[:], pt[:], AF.Identity, R=(pb,), W=(sb_,))
                    L.dma("st", o_v[s][t0 + sub * 128:t0 + (sub + 1) * 128, vg * 512:(vg + 1) * 512], st[:], R=(sb_,))
    return L


def build_B(Ts):
    L = Launch("B")
    ns = len(Ts)
    C = 128
    G = 4
    i_q = [L.din(f"qT{s}", [256, Ts[s]], F32) for s in range(ns)]
    i_k = [L.din(f"kT{s}", [256, Ts[s]], F32) for s in range(ns)]
    i_g = [L.din(f"gbT{s}", [256, Ts[s]], F32) for s in range(ns)]
    i_v = [L.din(f"v{s}", [Ts[s], 256], BF16) for s in range(ns)]
    i_l = [[L.din(f"l{'fb'[d]}T{s}", [256, Ts[s]], F32) for s in range(ns)] for d in range(2)]
    gnw = L.din("gnw", [128, 2], F32)
    ident = L.din("ident", [128, 128], BF16)
    masks = L.din("masks", [128, 2, 128], F32)
    ones = L.din("ones", [128, 128], BF16)
    o_mix = [L.dout(f"mixbT{s}", [256, Ts[s]], BF16) for s in range(ns)]
    maxch = max(Ts) // C
    sb_all = L.dint("sb_all", [maxch, 128, 2, 256], BF16)
    sbd = [Buf() for _ in range(maxch)]

    gn_t = L.sb([128, 2], F32)
    id_t = L.sb([128, 128], BF16)
    mk_t = L.sb([128, 2, 128], F32)
    on_t = L.sb([128, 128], BF16)
    on32 = L.sb([128, C], F32)
    b_c = Buf()
    b_on32 = Buf()
    L.dma("ld", gn_t[:], gnw, W=(b_c,))
    L.dma("ld", id_t[:], ident, W=(b_c,))
    L.dma("ld", mk_t[:], masks, W=(b_c,))
    L.dma("ld", on_t[:], ones, W=(b_c,))
    L.op("dve", lambda h: h.memset(on32[:], 1.0), W=(b_on32,))
    eps_t = L.sb([128, 1], F32)
    L.op("dve", lambda h: h.memset(eps_t[:], EPS), W=(b_on32,))

    S = [L.sb([128, 2, 256], F32) for _ in range(2)]
    Sb_ = [Buf(), Buf()]
    for d in range(2):
        L.op("dve", lambda h, d=d: h.memset(S[d][:], 0.0), W=(Sb_[d],))
    sbf = Ring(L, 3, [128, 2, 256], BF16)

    r_k = Ring(L, 2, [128, 2, G * C], F32)
    r_q = Ring(L, 2, [128, 2, G * C], F32)
    r_g = Ring(L, 2, [128, 2, G * C], F32)
    r_v = Ring(L, 2, [128, G, 256], BF16)
    r_lf = Ring(L, 2, [128, 2, G * C], F32)
    r_lb = Ring(L, 2, [128, 2, G * C], F32)
    r_sb = Ring(L, 3, [128, 2, 256], BF16)
    t32 = Ring(L, 12, [128, 2, C], F32)
    t16 = Ring(L, 12, [128, 2, C], BF16)
    tk = Ring(L, 3, [128, 256], BF16)
    sm = Ring(L, 6, [128, 2], F32)
    r_o = Ring(L, 3, [128, 2, C], BF16)
    r_r = Ring(L, 3, [128, C], F32)
    p_tr = Ring(L, 1, [128, 1024], BF16, psum=True)
    p_a = Ring(L, 2, [128, 512], F32, psum=True)
    p_o = Ring(L, 2, [128, 512], F32, psum=True)
    p_ms = Ring(L, 1, [128, 512], F32, psum=True)
    p_ds = Ring(L, 2, [128, 512], F32, psum=True)

    def v3(ap, a):
        return ap.rearrange("p (a b) -> p a b", a=a)

    def load_group(s, g0, want):
        T = Ts[s]
        t0 = g0 * C
        n = min(G * C, T - t0)
        out = {}
        for nm, ring, src in (("k", r_k, i_k[s]), ("q", r_q, i_q[s]), ("g", r_g, i_g[s]),
                              ("lf", r_lf, i_l[0][s]), ("lb", r_lb, i_l[1][s])):
            if nm in want:
                t, b = ring.next()
                L.dma("in", t[:, :, 0:n], src.rearrange("(h p) t -> p h t", p=128)[:, :, t0:t0 + n], W=(b,))
                out[nm] = (t, b)
        t, b = r_v.next()
        L.dma("in", t[:, 0:n // C, :], i_v[s].rearrange("(c p) d -> p c d", p=128)[:, g0:g0 + n // C, :], W=(b,))
        out["v"] = (t, b)
        return out

    def scan(l_ap, lb_):
        cs, cb = t32.next()
        for hf in range(2):
            L.op("dve", lambda h, hf=hf: h.tensor_tensor_scan(out=cs[:, hf, :], data0=on32[:, 0:C], data1=l_ap[:, hf, :],
                                                                initial=0.0, op0=ALU.mult, op1=ALU.add),
                 R=(lb_, b_on32), W=(cb,))
        return cs, cb

    def rsum(l_ap, lb_):
        cs, cb = scan(l_ap, lb_)
        d1, d1b = t32.next()
        tt(L, "dve", d1[:], l_ap, cs[:], ALU.subtract, R=(lb_, cb), W=(d1b,))
        for hf in range(2):
            ts(L, "dve", d1[:, hf, :], d1[:, hf, :], cs[:, hf, C - 1:C], ALU.add, R=(cb, d1b), W=(d1b,))
        return d1, d1b, cs, cb

    def expo(src, sb_in, scale):
        o, ob_ = t32.next()
        act(L, o[:], src, AF.Exp, R=(sb_in,), W=(ob_,), scale=scale)
        return o, ob_

    def mul16(en, a, ab, e, eb):
        o, ob_ = t16.next()
        tt(L, en, o[:], a, e[:], ALU.mult, R=(ab, eb), W=(ob_,))
        return o, ob_

    def ktilde_tm(kt, ktb):
        pt, pb = p_tr.next()
        pv = v3(pt[:, 0:256], 2)
        for hf in range(2):
            L.op("pe", lambda h, hf=hf: h.transpose(pv[:, hf, :], kt[:, hf, :], id_t[:]), R=(ktb, b_c), W=(pb,))
        o, ob_ = tk.next()
        act(L, o[:], pt[:, 0:256], AF.Identity, R=(pb,), W=(ob_,))
        return o, ob_

    def state_update(d, ktm, ktmb, v_ap, v_b, decs, dec_b):
        pt, pb = p_ds.next()
        pv = v3(pt[:], 2)
        for hf in range(2):
            mm(L, pv[:, hf, :], ktm[:, hf * 128:(hf + 1) * 128], v_ap, True, True, R=(ktmb, v_b), W=(pb,))
        tt(L, "dve", S[d][:], S[d][:], pv, ALU.add, R=(pb, Sb_[d]), W=(Sb_[d],))
        for hf in range(2):
            ts(L, "dve", S[d][:, hf, :], S[d][:, hf, :], decs[hf], ALU.mult, R=(dec_b, Sb_[d]), W=(Sb_[d],))

    for s in range(ns):
        T = Ts[s]
        nch = T // C
        order = sorted({(c // G) * G for c in range(nch)}, reverse=True)
        pend = {}

        def ensure1(gi, order=order, pend=pend, s=s):
            if gi < len(order) and gi not in pend:
                pend[gi] = load_group(s, order[gi], ("k", "lb"))

        for c in range(nch - 1, -1, -1):
            g0 = (c // G) * G
            gi = order.index(g0)
            ensure1(gi)
            ensure1(gi + 1)
            ld = pend[gi]
            j = c - g0
            sl = slice(j * C, (j + 1) * C)
            st, stb = sbf.next()
            act(L, st[:], S[1][:], AF.Identity, R=(Sb_[1],), W=(stb,))
            L.dma("sbw", sb_all[c], st[:], R=(stb,), W=(sbd[c],), sem_of=stb)
            lb_ap = ld["lb"][0][:, :, sl]
            rs, rsb, cs, cb = rsum(lb_ap, ld["lb"][1])
            em, emb = expo(rs[:], rsb, 1.0 / 16.0)
            dc, dcb = sm.next()
            act(L, dc[:], cs[:, :, C - 1], AF.Exp, R=(cb,), W=(dcb,), scale=-1.0 / 16.0)
            kt, ktb = mul16("dve", ld["k"][0][:, :, sl], ld["k"][1], em, emb)
            ktm, ktmb = ktilde_tm(kt, ktb)
            state_update(1, ktm, ktmb, ld["v"][0][:, j, :], ld["v"][1], [dc[:, 0:1], dc[:, 1:2]], dcb)
        order = sorted({(c // G) * G for c in range(nch)})
        pend = {}
        sbl = {}

        def ensure2(gi, order=order, pend=pend, s=s):
            if gi < len(order) and gi not in pend:
                pend[gi] = load_group(s, order[gi], ("k", "q", "g", "lf", "lb"))

        def ensure_sb(cc, sbl=sbl, nch=nch):
            if cc < nch and cc not in sbl:
                t_, b_ = r_sb.next()
                L.dma("in", t_[:], sb_all[cc], R=(sbd[cc],), W=(b_,))
                sbl[cc] = (t_, b_)

        for c in range(nch):
            g0 = (c // G) * G
            gi = order.index(g0)
            ensure2(gi)
            ensure2(gi + 1)
            ld = pend[gi]
            j = c - g0
            sl = slice(j * C, (j + 1) * C)
            ensure_sb(c)
            ensure_sb(c + 1)
            sbt, sbb = sbl.pop(c)
            k_ap, kb_ = ld["k"][0][:, :, sl], ld["k"][1]
            q_ap, qb_ = ld["q"][0][:, :, sl], ld["q"][1]
            v_ap, vb_ = ld["v"][0][:, j, :], ld["v"][1]
            csf, csfb = scan(ld["lf"][0][:, :, sl], ld["lf"][1])
            epf, epfb = expo(csf[:], csfb, -1.0 / 16.0)
            emf, emfb = expo(csf[:], csfb, 1.0 / 16.0)
            rs, rsb, cs, cb = rsum(ld["lb"][0][:, :, sl], ld["lb"][1])
            epb, epbb = expo(rs[:], rsb, -1.0 / 16.0)
            emb_, embb = expo(rs[:], rsb, 1.0 / 16.0)
            qf, qfb = mul16("pool", q_ap, qb_, epf, epfb)
            kf, kfb = mul16("dve", k_ap, kb_, emf, emfb)
            qb2, qbb = mul16("pool", q_ap, qb_, epb, epbb)
            kb2, kbb = mul16("dve", k_ap, kb_, emb_, embb)
            pa, pab = p_a.next()
            pav = v3(pa[:, 0:256], 2)
            for d, (kk, kkb, qq, qqb) in enumerate(((kf, kfb, qf, qfb), (kb2, kbb, qb2, qbb))):
                for hf in range(2):
                    mm(L, pav[:, d, :], kk[:, hf, :], qq[:, hf, :], hf == 0, hf == 1, R=(kkb, qqb), W=(pab,))
            at, atb = t16.next()
            tt(L, "dve", at[:], pav, mk_t[:], ALU.mult, R=(pab, b_c), W=(atb,))
            sf, sfb = sbf.next()
            act(L, sf[:], S[0][:], AF.Identity, R=(Sb_[0],), W=(sfb,))
            po, pob = p_o.next()
            pov = v3(po[:, 0:256], 2)
            for hv in range(2):
                hs = slice(hv * 128, (hv + 1) * 128)
                mm(L, pov[:, hv, :], v_ap[:, hs], at[:, 0, :], True, False, R=(vb_, atb), W=(pob,))
                mm(L, pov[:, hv, :], v_ap[:, hs], at[:, 1, :], False, False, R=(vb_, atb), W=(pob,))
                for hf in range(2):
                    mm(L, pov[:, hv, :], sf[:, hf, hs], qf[:, hf, :], False, False, R=(sfb, qfb), W=(pob,))
                for hf in range(2):
                    mm(L, pov[:, hv, :], sbt[:, hf, hs], qb2[:, hf, :], False, hf == 1, R=(sbb, qbb), W=(pob,))
            sq, sqb = t16.next()
            act(L, sq[:], pov, AF.Square, R=(pob,), W=(sqb,))
            pm, pmb = p_ms.next()
            for hv in range(2):
                mm(L, pm[:, 0:C], on_t[:], sq[:, hv, :], hv == 0, hv == 1, R=(b_c, sqb), W=(pmb,))
            sd, sdb = r_r.next()
            act(L, sd[:], pm[:, 0:C], AF.Sqrt, R=(pmb, b_on32), W=(sdb,), scale=1.0 / 256.0, bias=eps_t[:, 0:1])
            rr, rrb = r_r.next()
            L.op("dve", lambda h: h.reciprocal(out=rr[:], in_=sd[:]), R=(sdb,), W=(rrb,))
            y, yb = t32.next()
            for hv in range(2):
                tt(L, "dve", y[:, hv, :], pov[:, hv, :], rr[:], ALU.mult, R=(pob, rrb), W=(yb,))
            ot, otb = r_o.next()
            for hv in range(2):
                stt(L, ot[:, hv, :], y[:, hv, :], gn_t[:, hv:hv + 1], ld["g"][0][:, hv, sl], ALU.mult, ALU.mult,
                    R=(yb, b_c, ld["g"][1]), W=(otb,))
            L.dma("st", o_mix[s].rearrange("(h p) t -> p h t", p=128)[:, :, c * C:(c + 1) * C], ot[:], R=(otb,))
            ktm, ktmb = ktilde_tm(kf, kfb)
            state_update(0, ktm, ktmb, v_ap, vb_, [epf[:, 0, C - 1:C], epf[:, 1, C - 1:C]], epfb)
    return L


def build_C(Ts):
    L = Launch("C")
    ns = len(Ts)
    NT = 256
    mixT = [L.din(f"mixT{s}", [D, Ts[s]], BF16) for s in range(ns)]
    xTs = [L.din(f"xT{s}", [D, Ts[s]], F32) for s in range(ns)]
    wout = L.din("wout", [16, 128, 16, 128], F32)
    lnp = L.din("lnp", [128, 2, 16], F32)
    ones = L.din("ones", [128, 128], BF16)
    o_x = [L.dout(f"xn{s}", [D, Ts[s]], F32) for s in range(ns)]
    mt, mb, sc1, sb1, gta, gb2 = load_mod(L, ns)
    ln_t = L.sb([128, 2, 16], F32)
    on_t = L.sb([128, 128], BF16)
    eps_t = L.sb([128, 1], F32)
    b_c = Buf()
    L.dma("ld", ln_t[:], lnp, W=(b_c,))
    L.dma("ld", on_t[:], ones, W=(b_c,))
    b_e = Buf()
    L.op("dve", lambda h: h.memset(eps_t[:], EPS / (ALPHA * ALPHA)), W=(b_e,))
    wo = L.sb([128, 16, 16, 128], BF16)
    b_wo = [Buf() for _ in range(16)]
    stg = Ring(L, 2, [128, 16, 128], F32)
    for fc in range(16):
        st, sb_ = stg.next()
        L.dma("w", st[:], wout[fc], W=(sb_,))
        L.op("pool", lambda h, fc=fc, st=st: h.tensor_copy(out=wo[:, fc, :, :], in_=st[:]), R=(sb_,), W=(b_wo[fc],))
    r_mix = Ring(L, 2, [128, 16, NT], BF16)
    r_x = [(L.sb([128, 16, NT], F32), [Buf() for _ in range(16)]) for _ in range(2)]
    pr = Ring(L, 5, [128, 512], F32, psum=True)
    p_sum = Ring(L, 1, [128, 512], F32, psum=True)
    p_sq = Ring(L, 1, [128, 512], F32, psum=True)
    r_b = Ring(L, 6, [128, NT], BF16)
    r_s = Ring(L, 6, [128, NT], F32)
    r_d = Ring(L, 6, [128, NT], F32)
    tl = token_tiles(Ts, NT)
    loaded = {}

    def issue_loads(i):
        s, t0, n = tl[i]
        mx, mxb = r_mix.next()
        L.dma("in", mx[:, :, 0:n], mixT[s].rearrange("(c p) t -> p c t", p=128)[:, :, t0:t0 + n], W=(mxb,))
        xt, xbs = r_x[i % 2]
        L.dma("in", xt[:, :, 0:n], xTs[s].rearrange("(c p) t -> p c t", p=128)[:, :, t0:t0 + n], W=tuple(xbs))
        loaded[i] = (mx, mxb, xt, xbs)

    issue_loads(0)
    for i, (s, t0, n) in enumerate(tl):
        if i + 1 < len(tl):
            issue_loads(i + 1)
        mx, mxb, xt, xbs = loaded.pop(i)
        psu, psub = p_sum.next()
        psq, psqb = p_sq.next()
        for fc in range(16):
            xb = xbs[fc]
            pt, pb = pr.next()
            for kc in range(16):
                mm(L, pt[:, 0:n], wo[:, fc, kc, :], mx[:, kc, 0:n], kc == 0, kc == 15, R=(b_wo[fc], mxb), W=(pb,))
            stt(L, xt[:, fc, 0:n], pt[:, 0:n], gta[:, s, fc:fc + 1], xt[:, fc, 0:n], ALU.mult, ALU.add,
                R=(pb, gb2, xb), W=(xb,))
            tb, tbb = r_b.next()
            act(L, tb[:, 0:n], xt[:, fc, 0:n], AF.Identity, R=(xb,), W=(tbb,))
            tq, tqb = r_b.next()
            act(L, tq[:, 0:n], xt[:, fc, 0:n], AF.Square, R=(xb,), W=(tqb,))
            mm(L, psu[:, 0:n], on_t[:], tb[:, 0:n], fc == 0, fc == 15, R=(b_c, tbb), W=(psub,))
            mm(L, psq[:, 0:n], on_t[:], tq[:, 0:n], fc == 0, fc == 15, R=(b_c, tqb), W=(psqb,))
        mean, meanb = r_s.next()
        act(L, mean[:, 0:n], psu[:, 0:n], AF.Identity, R=(psub,), W=(meanb,), scale=1.0 / D)
        msq, msqb = r_s.next()
        tt(L, "dve", msq[:, 0:n], mean[:, 0:n], mean[:, 0:n], ALU.mult, R=(meanb,), W=(msqb,))
        var, varb = r_s.next()
        stt(L, var[:, 0:n], psq[:, 0:n], 1.0 / D, msq[:, 0:n], ALU.mult, ALU.subtract, R=(psqb, msqb), W=(varb,))
        sd, sdb = r_s.next()
        act(L, sd[:, 0:n], var[:, 0:n], AF.Sqrt, R=(varb, b_e), W=(sdb,), bias=eps_t[:, 0:1])
        rs, rsb = r_s.next()
        L.op("dve", lambda h, rs=rs, sd=sd, n=n: h.reciprocal(out=rs[:, 0:n], in_=sd[:, 0:n]), R=(sdb,), W=(rsb,))
        for fc in range(16):
            d1, d1b = r_d.next()
            en = "pool" if fc % 2 else "dve"
            tt(L, en, d1[:, 0:n], xt[:, fc, 0:n], mean[:, 0:n], ALU.subtract, R=(xbs[fc], meanb), W=(d1b,))
            tt(L, en, d1[:, 0:n], d1[:, 0:n], rs[:, 0:n], ALU.mult, R=(d1b, rsb), W=(d1b,))
            d2, d2b = r_d.next()
            act(L, d2[:, 0:n], d1[:, 0:n], AF.Identity, R=(d1b, b_c), W=(d2b,), scale=ln_t[:, 0, fc:fc + 1], bias=ln_t[:, 1, fc:fc + 1])
            L.dma("st", o_x[s][fc * 128:(fc + 1) * 128, t0:t0 + n], d2[:, 0:n], R=(d2b,))
    return L


def build_A_odd(Ts):
    L = Launch("A_odd")
    ns = len(Ts)
    xTs = [L.din(f"xT{s}", [D, Ts[s]], F32) for s in range(ns)]
    wfm = L.din("wfm", [32, 128, 16, 128], F32)
    o_xp = [L.dout(f"xpT{s}", [D, Ts[s]], F32) for s in range(ns)]
    o_sg = [L.dout(f"sgT{s}", [D, Ts[s]], F32) for s in range(ns)]
    mt, mb, sc1, sb1, gta, gb2 = load_mod(L, ns)
    tiles = token_tiles(Ts)
    passes = make_passes(tiles, 2304)
    CAP = max(sum(t[2] for t in p) for p in passes)
    uT = L.sb([128, 16, CAP], BF16)
    xr = Ring(L, 2, [128, 4, 512], F32)
    wp = WPipe(L, 4, 5, pf=3)
    pr = Ring(L, 6, [128, 512], F32, psum=True)
    o32 = Ring(L, 3, [128, 512], F32)
    o16 = Ring(L, 3, [128, 512], BF16)
    wp.set_plan([wfm[blk] for _ in passes for blk in range(32)])
    ub_all = [[Buf() for _ in range(16)] for _ in range(max(len(p) for p in passes))]
    for tiles_p in passes:
        ub = ub_all[:len(tiles_p)]
        build_uT(L, uT, ub, xTs, tiles_p, mt, mb, sc1, sb1, xr)
        offs = np.cumsum([0] + [t[2] for t in tiles_p])
        for blk in range(32):
            wt, wb = wp.get()
            for ti, (s, t0, n) in enumerate(tiles_p):
                o = int(offs[ti])
                pt, pb = pr.next()
                for kc in range(16):
                    mm(L, pt[:, 0:n], wt[:, kc, :], uT[:, kc, o:o + n], kc == 0, kc == 15, R=(wb, ub[ti][kc]), W=(pb,))
                if blk < 16:
                    st, sb_ = o32.next()
                    L.op("dve", lambda h, st=st, pt=pt, n=n: h.tensor_copy(out=st[:, 0:n], in_=pt[:, 0:n]), R=(pb,), W=(sb_,))
                    L.dma("st", o_xp[s][blk * 128:(blk + 1) * 128, t0:t0 + n], st[:, 0:n], R=(sb_,))
                else:
                    st, sb_ = o32.next()
                    act(L, st[:, 0:n], pt[:, 0:n], AF.Silu, R=(pb,), W=(sb_,))
                    L.dma("st", o_sg[s][(blk - 16) * 128:(blk - 15) * 128, t0:t0 + n], st[:, 0:n], R=(sb_,))
    return L


def build_P(Ts, RLs):
    L = Launch("P")
    ns = len(Ts)
    Rs = [Ts[s] // RLs[s] for s in range(ns)]
    xph = [L.din(f"xph{s}", [D, (Rs[s] + 16) * RLs[s]], F32) for s in range(ns)]
    invc = [L.din(f"invc{s}", [4, 128, Ts[s]], F32) for s in range(ns)]
    sgT = [L.din(f"sgT{s}", [D, Ts[s]], F32) for s in range(ns)]
    wpool = L.din("wpool", [16, 128, 4, 128], F32)
    pscale = L.din("pscale", [128, 16], F32)
    o_mix = [L.dout(f"mixT{s}", [D, Ts[s]], BF16) for s in range(ns)]
    ps_t = L.sb([128, 16], F32)
    b_ps = Buf()
    L.dma("ld", ps_t[:], pscale, W=(b_ps,))
    wpl = L.sb([128, 16, 4, 128], BF16)
    b_wp = [Buf() for _ in range(16)]
    stg = Ring(L, 2, [128, 4, 128], F32)
    for b in range(16):
        st, sb_ = stg.next()
        L.dma("w", st[:], wpool[b], W=(sb_,))
        L.op("pool", lambda h, b=b, st=st: h.tensor_copy(out=wpl[:, b, :, :], in_=st[:]), R=(sb_,), W=(b_wp[b],))
    NMAX = max((Rs[s] + 16) * RLs[s] for s in range(ns))
    TMAX = max(Ts)
    r_xp = Ring(L, 1, [128, NMAX], F32)
    r_sa = Ring(L, 2, [128, NMAX], F32)
    r_ic = Ring(L, 1, [128, TMAX], F32)
    r_z = Ring(L, 4, [128, TMAX], BF16)
    r_zt = Ring(L, 1, [128, TMAX], F32)
    pr = Ring(L, 4, [128, 512], F32, psum=True)
    r_sg = Ring(L, 3, [128, 512], F32)
    r_o = Ring(L, 3, [128, 512], BF16)
    for s in range(ns):
        T, RL, R = Ts[s], RLs[s], Rs[s]
        N = (R + 16) * RL
        for g, w in enumerate(POOL_WINDOWS):
            ic, icb = r_ic.next()
            L.dma("in", ic[:, 0:T], invc[s][g], W=(icb,))
            zs = []
            for kc in range(4):
                ch = g * 4 + kc
                xp, xpb = r_xp.next()
                L.dma("in", xp[:, 0:N], xph[s][ch * 128:(ch + 1) * 128, :], W=(xpb,))
                en = "dve" if kc % 2 == 0 else "pool"
                cur, curb = xp, xpb
                k = 1
                while k < w:
                    nx, nxb = r_sa.next()
                    m = N - k * RL
                    tt(L, en, nx[:, 0:m], cur[:, 0:m], cur[:, k * RL:k * RL + m], ALU.add, R=(curb,), W=(nxb,))
                    cur, curb = nx, nxb
                    k *= 2
                o0 = (8 - w // 2) * RL
                zt, ztb = r_zt.next()
                tt(L, en, zt[:, 0:T], cur[:, o0:o0 + T], ic[:, 0:T], ALU.mult, R=(curb, icb), W=(ztb,))
                z, zb = r_z.next()
                tt(L, en, z[:, 0:T], zt[:, 0:T], xp[:, 8 * RL:8 * RL + T], ALU.subtract, R=(ztb, xpb), W=(zb,))
                zs.append((z, zb))
            for oc in range(4):
                fc = g * 4 + oc
                for t0 in range(0, T, 512):
                    n = min(512, T - t0)
                    pt, pb = pr.next()
                    for kc in range(4):
                        mm(L, pt[:, 0:n], wpl[:, fc, kc, :], zs[kc][0][:, t0:t0 + n], kc == 0, kc == 3,
                           R=(b_wp[fc], zs[kc][1]), W=(pb,))
                    sg, sgb = r_sg.next()
                    L.dma("in", sg[:, 0:n], sgT[s][fc * 128:(fc + 1) * 128, t0:t0 + n], W=(sgb,))
                    ot, otb = r_o.next()
                    stt(L, ot[:, 0:n], pt[:, 0:n], ps_t[:, fc:fc + 1], sg[:, 0:n], ALU.mult, ALU.mult,
                        R=(pb, b_ps, sgb), W=(otb,))
                    L.dma("st", o_mix[s][fc * 128:(fc + 1) * 128, t0:t0 + n], ot[:, 0:n], R=(otb,))
    return L


_CACHE = {}


def _launch(key, builder, in_maps):
    if key not in _CACHE:
        L = builder()
        L.finish()
        _CACHE[key] = L
    L = _CACHE[key]
    res = run_bass_kernel_spmd(L.nc, in_maps, core_ids=list(range(len(in_maps))))
    return res.results


def _blk(W, c0, ncol=128):
    K = W.shape[0]
    return np.ascontiguousarray(W[:, c0:c0 + ncol].reshape(K // 128, 128, ncol).transpose(1, 0, 2))


def _pvec(v):
    return np.ascontiguousarray(v.reshape(-1, 128).T)


def kernel(x, c, ctx, c_ctx, w_ada, b_ada, ln_g, ln_b, w_in_e, w_gate_f, b_gate_f, w_gate_b, b_gate_b,
           gla_norm_w, conv_w, w_out_e, w_in_o, w_pool, pool_scale, w_out_o, _debug=None):
    f32 = np.float32
    x = np.asarray(x, f32)
    nb, seq, _ = x.shape
    lc = ctx.shape[1]
    cpb = NCORE // nb
    T = seq // cpb
    RPC = T // GRID_W
    rows_tot = seq // GRID_W

    cT = np.zeros((128, 16, 4), f32)
    rows = [np.asarray(c[0], f32), np.asarray(c[1], f32), np.asarray(c_ctx, f32)]
    for r in range(3):
        cT[:, :, r] = _pvec(rows[r])
    w_ada = np.asarray(w_ada, f32)
    b_ada = np.asarray(b_ada, f32)
    in_maps = []
    for cid in range(NCORE):
        wa = np.empty((DEPTH * 6, 128, 16, 128), f32)
        ba = np.empty((128, DEPTH * 6), f32)
        for l in range(DEPTH):
            for j in range(6):
                f = cid * 6 + j
                wa[l * 6 + j] = _blk(w_ada[l], f * 128)
                ba[:, l * 6 + j] = b_ada[l, f * 128:(f + 1) * 128]
        in_maps.append({"cT": cT, "wada": wa, "bada": ba})
    res = _launch("M", build_mod, in_maps)
    mfull = np.empty((DEPTH, 3, 3 * D), f32)
    for cid in range(NCORE):
        mo = res[cid]["modT"]
        for l in range(DEPTH):
            for j in range(6):
                f = cid * 6 + j
                mfull[l, :, f * 128:(f + 1) * 128] = mo[:, l * 6 + j, 0:3].T

    def mod_table(l, rows_):
        out = np.empty((128, len(rows_), 3, 16), f32)
        for s, r in enumerate(rows_):
            for j3 in range(3):
                out[:, s, j3, :] = _pvec(mfull[l, r, j3 * D:(j3 + 1) * D])
        return out

    xT = []
    for cid in range(NCORE):
        b, r = divmod(cid, cpb)
        xT.append(np.ascontiguousarray(x[b, r * T:(r + 1) * T, :].T))
    cxT = [np.ascontiguousarray(np.asarray(ctx[b], f32).T) for b in range(nb)]
    ident = np.eye(128, dtype=f32).astype(NPBF)
    ones = np.ones((128, 128), f32).astype(NPBF)
    jj, ii = np.meshgrid(np.arange(128), np.arange(128), indexing="ij")
    masks = np.stack([(jj <= ii), (jj >= ii)], axis=1).astype(f32)

    for i in range(DEPTH):
        j = i // 2
        ctx_needed = any(l % 2 == 0 for l in range(i + 1, DEPTH))
        if i % 2 == 0:
            W = np.asarray(w_in_e[j], f32)
            order = [k * 128 for k in range(8)] + [2080 + k * 128 for k in range(8)] + [3104 + k * 128 for k in range(8)]
            for cc in range(8):
                order += [4128 + cc * 128, 5152 + cc * 128, 6176 + cc * 128, 7200 + cc * 128]
            order += [1024 + k * 128 for k in range(8)]
            wfm = np.stack([_blk(W, c0) for c0 in order])
            wr = _blk(W, 2048, 32)
            wg = np.ascontiguousarray(np.stack([np.asarray(w_gate_f[j], f32), np.asarray(w_gate_b[j], f32)], axis=1))
            bg = np.ascontiguousarray(np.stack([_pvec(np.asarray(b_gate_f[j], f32)), _pvec(np.asarray(b_gate_b[j], f32))], axis=1))
            cwv = np.asarray(conv_w[j], f32)
            convw = np.ascontiguousarray(np.stack([_pvec(cwv[k]) for k in range(3)], axis=1))
            Ts = [T, lc]
            in_maps = []
            for cid in range(NCORE):
                b = cid // cpb
                in_maps.append({"xT0": xT[cid], "xT1": cxT[b], "mod": mod_table(i, [b, 2]), "wfm": wfm, "wr": wr,
                                "wg": wg, "bg": bg, "convw": convw})
            ra = _launch(("Ae", T, lc), lambda: build_A_even(Ts, [GRID_W, lc]), in_maps)
            if _debug is not None:
                _debug[f"A{i}"] = ra
            in_maps = []
            gnw = _pvec(np.asarray(gla_norm_w[j], f32))
            for cid in range(NCORE):
                b, h = divmod(cid, H)
                hs = slice(h * 256, (h + 1) * 256)
                m = {"gnw": gnw, "ident": ident, "masks": masks, "ones": ones}
                src0 = ra[b * cpb]
                for nm in ("qT", "kT", "gbT", "lfT", "lbT"):
                    m[nm + "0"] = np.ascontiguousarray(src0[nm + "1"][hs, :])
                    m[nm + "1"] = np.ascontiguousarray(np.concatenate([ra[b * cpb + r][nm + "0"][hs, :] for r in range(cpb)], axis=1))
                m["v0"] = np.ascontiguousarray(src0["v1"][:, hs])
                m["v1"] = np.ascontiguousarray(np.concatenate([ra[b * cpb + r]["v0"][:, hs] for r in range(cpb)], axis=0))
                in_maps.append(m)
            rb = _launch(("B", lc, seq), lambda: build_B([lc, seq]), in_maps)
            if _debug is not None:
                _debug[f"B{i}"] = rb
            mix_main, mix_ctx = [], []
            for cid in range(NCORE):
                b, r = divmod(cid, cpb)
                top = np.concatenate([rb[b * H + h]["mixbT1"][:, r * T:(r + 1) * T] for h in range(H)], axis=0)
                mix_main.append(np.ascontiguousarray(np.concatenate([top, ra[cid]["yaT0"]], axis=0)))
                topc = np.concatenate([rb[b * H + h]["mixbT0"] for h in range(H)], axis=0)
                mix_ctx.append(np.ascontiguousarray(np.concatenate([topc, ra[cid]["yaT1"]], axis=0)))
            w_out = np.asarray(w_out_e[j], f32)
        else:
            W = np.asarray(w_in_o[j], f32)
            wfm = np.stack([_blk(W, k * 128) for k in range(32)])
            Ts = [T, lc] if ctx_needed else [T]
            in_maps = []
            for cid in range(NCORE):
                b = cid // cpb
                m = {"xT0": xT[cid], "mod": mod_table(i, [b, 2] if ctx_needed else [b]), "wfm": wfm}
                if ctx_needed:
                    m["xT1"] = cxT[b]
                in_maps.append(m)
            ra = _launch(("Ao", tuple(Ts)), lambda: build_A_odd(Ts), in_maps)
            if _debug is not None:
                _debug[f"A{i}"] = ra
            wp = np.asarray(w_pool[j], f32)
            wpool = np.stack([_blk(wp[g], oc * 128) for g in range(4) for oc in range(4)])
            pscale = _pvec(np.asarray(pool_scale[j], f32))
            RLs = [GRID_W, 1] if ctx_needed else [GRID_W]

            def invcnt(rtot, r0, nr, rl):
                out = np.empty((4, 128, nr * rl), f32)
                rr = np.arange(r0, r0 + nr)
                for g, w in enumerate(POOL_WINDOWS):
                    lo = np.clip(rr - w // 2, 0, rtot)
                    hi = np.clip(rr + w - w // 2, 0, rtot)
                    out[g] = np.repeat(1.0 / (hi - lo).astype(f32), rl)[None, :]
                return out

            in_maps = []
            for cid in range(NCORE):
                b, r = divmod(cid, cpb)
                cur = ra[cid]["xpT0"]
                hw = 8 * GRID_W
                left = ra[cid - 1]["xpT0"][:, T - hw:] if r > 0 else np.zeros((D, hw), f32)
                right = ra[cid + 1]["xpT0"][:, :hw] if r < cpb - 1 else np.zeros((D, hw), f32)
                m = {"xph0": np.ascontiguousarray(np.concatenate([left, cur, right], axis=1)),
                     "invc0": invcnt(rows_tot, r * RPC, RPC, GRID_W), "sgT0": ra[cid]["sgT0"],
                     "wpool": wpool, "pscale": pscale}
                if ctx_needed:
                    z8 = np.zeros((D, 8), f32)
                    m["xph1"] = np.ascontiguousarray(np.concatenate([z8, ra[cid]["xpT1"], z8], axis=1))
                    m["invc1"] = invcnt(lc, 0, lc, 1)
                    m["sgT1"] = ra[cid]["sgT1"]
                in_maps.append(m)
            rp = _launch(("P", tuple(Ts)), lambda: build_P(Ts, RLs), in_maps)
            if _debug is not None:
                _debug[f"P{i}"] = rp
            mix_main = [rp[cid]["mixT0"] for cid in range(NCORE)]
            mix_ctx = [rp[cid]["mixT1"] for cid in range(NCORE)] if ctx_needed else None
            w_out = np.asarray(w_out_o[j], f32)
        wout = np.stack([_blk(w_out, fc * 128) for fc in range(16)])
        lnp = np.ascontiguousarray(np.stack([_pvec(np.asarray(ln_g[i], f32)), _pvec(np.asarray(ln_b[i], f32))], axis=1))
        Ts = [T, lc] if ctx_needed else [T]
        in_maps = []
        for cid in range(NCORE):
            b = cid // cpb
            m = {"mixT0": mix_main[cid], "xT0": xT[cid], "mod": mod_table(i, [b, 2] if ctx_needed else [b]),
                 "wout": wout, "lnp": lnp, "ones": ones}
            if ctx_needed:
                m["mixT1"] = mix_ctx[cid]
                m["xT1"] = cxT[b]
            in_maps.append(m)
        rc = _launch(("C", tuple(Ts)), lambda: build_C(Ts), in_maps)
        xT = [rc[cid]["xn0"] for cid in range(NCORE)]
        if ctx_needed:
            cxT = [rc[b * cpb]["xn1"] for b in range(nb)]
        if _debug is not None:
            _debug[f"x{i}"] = np.stack([np.concatenate([xT[b * cpb + r].T for r in range(cpb)], axis=0) for b in range(nb)])
            _debug[f"ctx{i}"] = np.stack([cxT[b].T for b in range(nb)])

    out = np.empty((nb, seq, D), f32)
    for cid in range(NCORE):
        b, r = divmod(cid, cpb)
        out[b, r * T:(r + 1) * T, :] = xT[cid].T
    return out
```
